# Optimizing a Trainium2 kernel written in Bass

```python
import jax, jax.numpy as jnp
from jax import lax
import numpy as np

D_MODEL = 1024
BATCH = 2
SEQ = 8192
DEPTH = 1

CHUNK = 64
Q_BLOCK = 128
N_MEM = 256
EPS = 1e-6

SSD_HEADS = 16
SSD_HEAD_DIM = 64
SSD_D_INNER = SSD_HEADS * SSD_HEAD_DIM
SSD_GROUPS = 4
SSD_STATE = 128
SSD_CONV = 4
SSD_XBC = SSD_D_INNER + 2 * SSD_GROUPS * SSD_STATE

MLA_HEADS = 16
MLA_Q_RANK = 384
MLA_KV_RANK = 256
MLA_NOPE = 64
MLA_ROPE = 32
MLA_V = 64
ROPE_THETA = 10000.0

MIX_WIDTH = SSD_D_INNER + MLA_HEADS * MLA_V
IN_SSD = SSD_D_INNER + SSD_XBC + SSD_HEADS
IN_MLA = MLA_Q_RANK + MLA_KV_RANK + MLA_ROPE
IN_WIDTH = IN_SSD + IN_MLA

MEM_HEADS = 4
MEM_HEAD_DIM = D_MODEL // MEM_HEADS

D_FF = 2816
FFN_CONV = 3

kernel_name = "hybrid_ssd_mla_memory_convffn_block"


def rmsnorm(x, w):
    xf = x.astype(jnp.float32)
    y = xf * lax.rsqrt(jnp.mean(xf * xf, axis=-1, keepdims=True) + EPS)
    return (y * w.astype(jnp.float32)).astype(x.dtype)


def causal_depthwise_conv(x, w, b):
    k = w.shape[0]
    y = lax.conv_general_dilated(
        x, w[:, None, :].astype(x.dtype), window_strides=(1,), padding=[(k - 1, 0)],
        dimension_numbers=("NWC", "WIO", "NWC"), feature_group_count=x.shape[-1])
    return y + b.astype(x.dtype)


def segsum_exp(a_cum):
    l = a_cum.shape[-1]
    diff = a_cum[..., :, None] - a_cum[..., None, :]
    mask = jnp.tril(jnp.ones((l, l), dtype=bool))
    return jnp.where(mask, jnp.exp(jnp.where(mask, diff, 0.0)), 0.0)


def rope_cos_sin(positions, dim):
    inv_freq = 1.0 / (ROPE_THETA ** (jnp.arange(0, dim, 2, dtype=jnp.float32) / dim))
    ang = positions.astype(jnp.float32)[..., None] * inv_freq
    return jnp.cos(ang), jnp.sin(ang)


def apply_rope(t, cos, sin):
    t1, t2 = jnp.split(t.astype(jnp.float32), 2, axis=-1)
    return jnp.concatenate([t1 * cos - t2 * sin, t1 * sin + t2 * cos], axis=-1).astype(t.dtype)


def ssd_mixer(u, conv_w, conv_b, dt_bias, a_log, d_skip, norm_w):
    b, s, _ = u.shape
    nc = s // CHUNK
    e = SSD_HEADS // SSD_GROUPS
    z, xbc, dt = jnp.split(u, [SSD_D_INNER, SSD_D_INNER + SSD_XBC], axis=-1)
    xbc = jax.nn.silu(causal_depthwise_conv(xbc, conv_w, conv_b))
    xs, bm, cm = jnp.split(xbc, [SSD_D_INNER, SSD_D_INNER + SSD_GROUPS * SSD_STATE], axis=-1)
    xs = xs.reshape(b, nc, CHUNK, SSD_GROUPS, e, SSD_HEAD_DIM)
    bm = bm.reshape(b, nc, CHUNK, SSD_GROUPS, SSD_STATE)
    cm = cm.reshape(b, nc, CHUNK, SSD_GROUPS, SSD_STATE)
    dt = jax.nn.softplus(dt.astype(jnp.float32) + dt_bias.astype(jnp.float32))
    a = -jnp.exp(a_log.astype(jnp.float32))
    dt_c = dt.reshape(b, nc, CHUNK, SSD_GROUPS, e)
    a_dt = jnp.transpose(dt_c * a.reshape(SSD_GROUPS, e), (0, 3, 4, 1, 2))
    a_cum = jnp.cumsum(a_dt, axis=-1)
    x_dt = xs * dt_c[..., None]
    decay_in = segsum_exp(a_cum)
    cb = jnp.einsum("bclgn,bcsgn->bcgls", cm, bm)
    y_diag = jnp.einsum("bcgls,bgecls,bcsgep->bclgep", cb, decay_in, x_dt)
    decay_states = jnp.exp(a_cum[..., -1:] - a_cum)
    states = jnp.einsum("bclgn,bgecl,bclgep->bcgepn", bm, decay_states, x_dt)
    chunk_decay = jnp.exp(a_cum[..., -1])

    def step(h, inp):
        st, dec = inp
        return h * dec[..., None, None] + st, h

    h0 = jnp.zeros_like(states[:, 0])
    _, prev = lax.scan(step, h0, (jnp.moveaxis(states, 1, 0), jnp.moveaxis(chunk_decay, 3, 0)))
    prev = jnp.moveaxis(prev, 0, 1)
    y_off = jnp.einsum("bclgn,bcgepn,bgecl->bclgep", cm, prev, jnp.exp(a_cum))
    y = (y_diag + y_off).reshape(b, s, SSD_HEADS, SSD_HEAD_DIM)
    y = y + xs.reshape(b, s, SSD_HEADS, SSD_HEAD_DIM) * d_skip[:, None]
    y = y.reshape(b, s, SSD_D_INNER) * jax.nn.silu(z)
    y = rmsnorm(y.reshape(b, s, SSD_GROUPS, SSD_D_INNER // SSD_GROUPS),
                norm_w.reshape(SSD_GROUPS, SSD_D_INNER // SSD_GROUPS))
    return y.reshape(b, s, SSD_D_INNER).astype(u.dtype)


def mla_mixer(u, positions, q_norm_w, w_q_up, kv_norm_w, w_kv_up):
    b, s, _ = u.shape
    q_lat, kv_lat, k_rope = jnp.split(u, [MLA_Q_RANK, MLA_Q_RANK + MLA_KV_RANK], axis=-1)
    q = (rmsnorm(q_lat, q_norm_w) @ w_q_up).reshape(b, s, MLA_HEADS, MLA_NOPE + MLA_ROPE)
    q_nope, q_rope = jnp.split(q, [MLA_NOPE], axis=-1)
    kv = (rmsnorm(kv_lat, kv_norm_w) @ w_kv_up).reshape(b, s, MLA_HEADS, MLA_NOPE + MLA_V)
    k_nope, v = jnp.split(kv, [MLA_NOPE], axis=-1)
    cos, sin = rope_cos_sin(positions, MLA_ROPE)
    q_rope = apply_rope(q_rope, cos[:, :, None, :], sin[:, :, None, :])
    k_rope = apply_rope(k_rope, cos, sin)
    scale = (MLA_NOPE + MLA_ROPE) ** -0.5
    nb = s // Q_BLOCK
    k_chunk = jnp.arange(s) // CHUNK

    def blocks(t):
        return jnp.moveaxis(t.reshape(b, nb, Q_BLOCK, *t.shape[2:]), 1, 0)

    def attend(args):
        i, qn, qr = args
        sc = jnp.einsum("bqhd,bkhd->bhqk", qn, k_nope) + jnp.einsum("bqhr,bkr->bhqk", qr, k_rope)
        sc = sc.astype(jnp.float32) * scale
        q_chunk = (i * Q_BLOCK + jnp.arange(Q_BLOCK)) // CHUNK
        mask = k_chunk[None, :] <= q_chunk[:, None]
        sc = jnp.where(mask, sc, jnp.finfo(jnp.float32).min)
        p = jax.nn.softmax(sc, axis=-1).astype(v.dtype)
        return jnp.einsum("bhqk,bkhd->bqhd", p, v)

    o = lax.map(attend, (jnp.arange(nb), blocks(q_nope), blocks(q_rope)))
    return jnp.moveaxis(o, 0, 1).reshape(b, s, MLA_HEADS * MLA_V).astype(u.dtype)


def memory_attention(h, mem_h, wq, wk, wv, wo):
    b, s, _ = h.shape
    q = (h @ wq).reshape(b, s, MEM_HEADS, MEM_HEAD_DIM)
    k = (mem_h @ wk).reshape(b, -1, MEM_HEADS, MEM_HEAD_DIM)
    v = (mem_h @ wv).reshape(b, -1, MEM_HEADS, MEM_HEAD_DIM)
    sc = jnp.einsum("bqhd,bkhd->bhqk", q, k).astype(jnp.float32) * (MEM_HEAD_DIM ** -0.5)
    p = jax.nn.softmax(sc, axis=-1).astype(v.dtype)
    o = jnp.einsum("bhqk,bkhd->bqhd", p, v).reshape(b, s, D_MODEL)
    return o @ wo


def conv_ffn(h, w_up, conv_w, conv_b, w_down):
    u = causal_depthwise_conv(h @ w_up, conv_w, conv_b)
    g, val = jnp.split(u, 2, axis=-1)
    return (jax.nn.silu(g) * val) @ w_down


def setup_inputs(seed: int = 0) -> dict:
    key = jax.random.key(seed)
    ks = iter(jax.random.split(key, 40))
    f32 = jnp.float32

    def dense(shape, fan_in):
        return jax.random.normal(next(ks), shape, f32) * (fan_in ** -0.5)

    def gain(shape):
        return 1.0 + 0.02 * jax.random.normal(next(ks), shape, f32)

    def small(shape):
        return 0.01 * jax.random.normal(next(ks), shape, f32)

    L = DEPTH
    x = jax.random.normal(next(ks), (BATCH, SEQ, D_MODEL), f32)
    mem = jax.random.normal(next(ks), (BATCH, N_MEM, D_MODEL), f32)
    offset = jax.random.randint(next(ks), (BATCH, 1), 0, 1024, dtype=jnp.int32)
    positions = offset + jnp.arange(SEQ, dtype=jnp.int32)[None, :]
    dt0 = jnp.exp(jax.random.uniform(next(ks), (L, SSD_HEADS), f32, np.log(1e-3), np.log(1e-1)))
    dt_bias = dt0 + jnp.log(-jnp.expm1(-dt0))
    a_log = jnp.log(jax.random.uniform(next(ks), (L, SSD_HEADS), f32, 1.0, 16.0))
    return {
        "x": x,
        "mem": mem,
        "positions": positions,
        "norm_mix_w": gain((L, D_MODEL)),
        "w_in": dense((L, D_MODEL, IN_WIDTH), D_MODEL),
        "ssd_conv_w": dense((L, SSD_CONV, SSD_XBC), SSD_CONV),
        "ssd_conv_b": small((L, SSD_XBC)),
        "ssd_dt_bias": dt_bias,
        "ssd_a_log": a_log,
        "ssd_d": 1.0 + 0.1 * jax.random.normal(next(ks), (L, SSD_HEADS), f32),
        "ssd_norm_w": gain((L, SSD_D_INNER)),
        "mla_q_norm_w": gain((L, MLA_Q_RANK)),
        "mla_w_q_up": dense((L, MLA_Q_RANK, MLA_HEADS * (MLA_NOPE + MLA_ROPE)), MLA_Q_RANK),
        "mla_kv_norm_w": gain((L, MLA_KV_RANK)),
        "mla_w_kv_up": dense((L, MLA_KV_RANK, MLA_HEADS * (MLA_NOPE + MLA_V)), MLA_KV_RANK),
        "w_out": dense((L, MIX_WIDTH, D_MODEL), MIX_WIDTH),
        "norm_memx_w": gain((L, D_MODEL)),
        "norm_mem_w": gain((L, D_MODEL)),
        "w_mem_q": dense((L, D_MODEL, D_MODEL), D_MODEL),
        "w_mem_k": dense((L, D_MODEL, D_MODEL), D_MODEL),
        "w_mem_v": dense((L, D_MODEL, D_MODEL), D_MODEL),
        "w_mem_o": dense((L, D_MODEL, D_MODEL), D_MODEL),
        "norm_ffn_w": gain((L, D_MODEL)),
        "w_ffn_up": dense((L, D_MODEL, 2 * D_FF), D_MODEL),
        "ffn_conv_w": dense((L, FFN_CONV, 2 * D_FF), FFN_CONV),
        "ffn_conv_b": small((L, 2 * D_FF)),
        "w_ffn_down": dense((L, D_FF, D_MODEL), D_FF),
        "norm_final_w": gain((D_MODEL,)),
    }


def reference(x, mem, positions, norm_mix_w, w_in, ssd_conv_w, ssd_conv_b, ssd_dt_bias,
              ssd_a_log, ssd_d, ssd_norm_w, mla_q_norm_w, mla_w_q_up, mla_kv_norm_w,
              mla_w_kv_up, w_out, norm_memx_w, norm_mem_w, w_mem_q, w_mem_k, w_mem_v,
              w_mem_o, norm_ffn_w, w_ffn_up, ffn_conv_w, ffn_conv_b, w_ffn_down, norm_final_w):
    for l in range(DEPTH):
        h = rmsnorm(x, norm_mix_w[l])
        u = h @ w_in[l]
        u_ssd, u_mla = jnp.split(u, [IN_SSD], axis=-1)
        y_ssd = ssd_mixer(u_ssd, ssd_conv_w[l], ssd_conv_b[l], ssd_dt_bias[l], ssd_a_log[l],
                          ssd_d[l], ssd_norm_w[l])
        y_mla = mla_mixer(u_mla, positions, mla_q_norm_w[l], mla_w_q_up[l],
                          mla_kv_norm_w[l], mla_w_kv_up[l])
        x = x + jnp.concatenate([y_ssd, y_mla], axis=-1) @ w_out[l]
        x = x + memory_attention(rmsnorm(x, norm_memx_w[l]), rmsnorm(mem, norm_mem_w[l]),
                                 w_mem_q[l], w_mem_k[l], w_mem_v[l], w_mem_o[l])
        x = x + conv_ffn(rmsnorm(x, norm_ffn_w[l]), w_ffn_up[l], ffn_conv_w[l], ffn_conv_b[l],
                         w_ffn_down[l])
    return rmsnorm(x, norm_final_w)
```

```python
import math
from contextlib import ExitStack

import numpy as np
import ml_dtypes

import concourse.bass as bass
import concourse.mybir as mybir
from concourse.bass_utils import run_bass_kernel_spmd

F32 = mybir.dt.float32
BF16 = mybir.dt.bfloat16
I32 = mybir.dt.int32
AF = mybir.ActivationFunctionType
ALU = mybir.AluOpType

D_MODEL = 1024
N_MEM = 256
EPS = 1e-6
D_FF = 2816
import os
NO_SELF_WAIT = os.environ.get('NO_SELF_WAIT', '0') == '1'
TWO_PI = 2.0 * math.pi
C1A = 6.28125
C1B = TWO_PI - 6.28125


SEM_EPOCH = 1200


class Sem:
    def __init__(self, nc, es, name, owner=None):
        self.h = es.enter_context(nc.semaphore(name))
        self.count = 0
        self.owner = owner


class Eng:
    def __init__(self, nc, es, name, handle):
        self.h = handle
        self.name = name
        self.nc = nc
        self.es = es
        self.nsem = 0
        self.sem = Sem(nc, es, "s_" + name, owner=self)
        self.seen = {}
        self.n = 0

    def rotate(self):
        if self.sem.count >= SEM_EPOCH:
            self.nsem += 1
            self.sem = Sem(self.nc, self.es, f"s_{self.name}_{self.nsem}", owner=self)


class Buf:
    __slots__ = ("name", "w", "r", "excl")

    def __init__(self, name="", excl=False):
        self.name = name
        self.w = None
        self.r = {}
        self.excl = excl


class Trk:
    def __init__(self, nc, es):
        self.nc = nc
        self.es = es
        self.pe = Eng(nc, es, "pe", nc.tensor)
        self.act = Eng(nc, es, "act", nc.scalar)
        self.dve = Eng(nc, es, "dve", nc.vector)
        self.pool = Eng(nc, es, "pool", nc.gpsimd)
        self.sp = Eng(nc, es, "sp", nc.sync)
        self.engs = [self.pe, self.act, self.dve, self.pool, self.sp]
        self.nsem = 0

    def newsem(self):
        self.nsem += 1
        return Sem(self.nc, self.es, f"d{self.nsem}")

    def _wait(self, eng, deps):
        best = {}
        for (s, v) in deps:
            if s.owner is eng and (eng is self.pe or NO_SELF_WAIT):
                continue
            if best.get(s, 0) < v:
                best[s] = v
        for s, v in best.items():
            if eng.seen.get(s, 0) < v:
                eng.h.wait_ge(s.h, v)
                eng.seen[s] = v

    @staticmethod
    def _deps(reads, writes, own=None):
        deps = []
        for b in reads:
            if b.w is not None:
                deps.append(b.w)
            if b.excl:
                deps.extend((s, v) for s, v in b.r.items() if s.owner is not own)
        for b in writes:
            if b.w is not None:
                deps.append(b.w)
            deps.extend(b.r.items())
        return deps

    @staticmethod
    def _mark(tok, reads, writes):
        s, v = tok
        for b in reads:
            if b.r.get(s, 0) < v:
                b.r[s] = v
        for b in writes:
            b.w = tok
            b.r = {}

    def op(self, eng, fn, reads=(), writes=()):
        reads = [getattr(b, "buf", b) for b in reads]
        writes = [getattr(b, "buf", b) for b in writes]
        self._wait(eng, self._deps(reads, writes, eng))
        inst = fn(eng.h)
        eng.rotate()
        eng.sem.count += 1
        eng.n += 1
        inst.then_inc(eng.sem.h, 1)
        tok = (eng.sem, eng.sem.count)
        self._mark(tok, reads, writes)
        return tok

    def dma(self, q, sem, out, in_, reads=(), writes=(), **kw):
        reads = [getattr(b, "buf", b) for b in reads]
        writes = [getattr(b, "buf", b) for b in writes]
        self._wait(q, self._deps(reads, writes))
        inst = q.h.dma_start(out=out, in_=in_, **kw)
        sem.count += 16
        q.n += 1
        inst.then_inc(sem.h, 16)
        tok = (sem, sem.count)
        self._mark(tok, reads, writes)
        return tok

    def wait_all(self, eng, bufs):
        deps = []
        for b in bufs:
            b = getattr(b, "buf", b)
            if b.w is not None:
                deps.append(b.w)
            deps.extend(b.r.items())
        self._wait(eng, deps)

    def barrier(self, extra_sems=()):
        toks = [(e.sem, e.sem.count) for e in self.engs if e.sem.count > 0]
        toks += [(s, s.count) for s in extra_sems if s.count > 0]
        for e in self.engs:
            self._wait(e, [t for t in toks if t[0].owner is not e])


class Tl:
    def __init__(self, t, name=""):
        self.t = t
        self.buf = Buf(name)

    def __getitem__(self, k):
        return self.t[k]


class KB:
    def __init__(self, nc, es):
        self.nc = nc
        self.es = es
        self.T = Trk(nc, es)
        self.dsems = []

    def sb(self, name, shape, dt=F32):
        return Tl(self.es.enter_context(self.nc.sbuf_tensor("sb_" + name, list(shape), dt)), name)

    def psum(self, name, shape, dt=F32):
        tl = Tl(self.es.enter_context(self.nc.psum_tensor("ps_" + name, list(shape), dt)), name)
        tl.buf.excl = True
        return tl

    def dsem(self):
        s = self.T.newsem()
        self.dsems.append(s)
        return s


def bc(ap, axis, shape):
    return ap.unsqueeze(axis).broadcast_to(list(shape))


def phase1(K, S, d, yx, dbg=None, stages="RWABCDEFGO"):
    nc, T = K.nc, K.T
    pe, act, dve, pool, sp = T.pe, T.act, T.dve, T.pool, T.sp
    NT = S // 128
    NS = S // 512
    SQ = S // 4
    es = K.es

    ident_f = K.sb("ident_f", [128, 128]); ident_b = K.sb("ident_b", [128, 128], BF16)
    tri_le = K.sb("tri_le", [128, 128]); tri_gt = K.sb("tri_gt", [128, 128])
    ones_f = K.sb("ones_f", [128, 128]); ones_b = K.sb("ones_b", [128, 128], BF16)
    cs = K.dsem()
    T.dma(sp, cs, ident_f[:], d["c_ident"][:, :], writes=[ident_f])
    T.dma(sp, cs, tri_le[:], d["c_tri_le"][:, :], writes=[tri_le])
    T.dma(sp, cs, tri_gt[:], d["c_tri_gt"][:, :], writes=[tri_gt])
    T.op(pool, lambda e: e.memset(ones_f[:], 1.0), writes=[ones_f])
    T.op(pool, lambda e: e.memset(ones_b[:], 1.0), writes=[ones_b])

    nmw = K.sb("nmw", [128, 8]); qnw = K.sb("qnw", [128, 3]); kvnw = K.sb("kvnw", [128, 2])
    cw = K.sb("cw", [128, 4, 4]); cb = K.sb("cb", [128, 4])
    dtb = K.sb("dtb", [128, 4]); aneg = K.sb("aneg", [128, 4]); dsk = K.sb("dsk", [128, 4])
    snw = K.sb("snw", [128, 256]); invf = K.sb("invf", [128, 16])
    posi = K.sb("posi", [128, NT], I32)
    T.dma(sp, cs, nmw[:], d["nmw"][:, :], writes=[nmw])
    T.dma(sp, cs, qnw[:], d["qnw"][:, :], writes=[qnw])
    T.dma(sp, cs, kvnw[:], d["kvnw"][:, :], writes=[kvnw])
    T.dma(sp, cs, cw[:], d["cw"][:, :, :], writes=[cw])
    T.dma(sp, cs, cb[:], d["cb"][:, :], writes=[cb])
    T.dma(sp, cs, dtb[:], d["dtb"][:, :], writes=[dtb])
    T.dma(sp, cs, aneg[:], d["alog"][:, :], writes=[aneg])
    T.dma(sp, cs, dsk[:], d["dsk"][:, :], writes=[dsk])
    T.dma(sp, cs, snw[:], d["snw"][:, :], writes=[snw])
    T.dma(sp, cs, invf[:], d["invf"][:, :], writes=[invf])
    T.dma(sp, cs, posi[:], d["pos"][:, :], writes=[posi])
    for tl_ in (ident_f, tri_le, tri_gt, nmw, qnw, kvnw, cw, cb, dtb, aneg, dsk, snw, invf, posi):
        tl_.buf.w = (cs, cs.count)
    T.op(dve, lambda e: e.tensor_copy(out=ident_b[:], in_=ident_f[:]), reads=[ident_f], writes=[ident_b])
    if "a" not in stages:
      T.op(act, lambda e: e.activation(out=aneg[:], in_=aneg[:], func=AF.Exp), reads=[aneg], writes=[aneg])
    T.op(dve, lambda e: e.tensor_scalar(out=aneg[:], in0=aneg[:], scalar1=-1.0, scalar2=None, op0=ALU.mult),
         reads=[aneg], writes=[aneg])

    sin_t = K.sb("sin_t", [128, NT, 16]); cos_t = K.sb("cos_t", [128, NT, 16])
    with ExitStack() as tes:
      if "R" in stages:
          tsb = lambda name, shape, dt=F32: Tl(tes.enter_context(nc.sbuf_tensor("sbt_" + name, list(shape), dt)), name)
          import re as _re
          _m = _re.search(r"R(\d+)", stages); _lim = int(_m.group(1)) if _m else 999; _cnt = [0]
          def rop_op(*a, **k):
              _cnt[0] += 1
              if _cnt[0] <= _lim:
                  return T.op(*a, **k)
          posf = tsb("posf", [128, NT]); ang = tsb("ang", [128, NT, 16]); kf = tsb("kf", [128, NT, 16])
          ki = tsb("ki", [128, NT, 16], I32); rr = tsb("rr", [128, NT, 16]); rc = tsb("rc", [128, NT, 16])
          mm = tsb("mm", [128, NT, 16])
          rop_op(dve, lambda e: e.tensor_copy(out=posf[:], in_=posi[:]), reads=[posi], writes=[posf])
          rop_op(dve, lambda e: e.tensor_tensor(out=ang[:], in0=bc(posf[:], 2, [128, NT, 16]),
                                              in1=bc(invf[:], 1, [128, NT, 16]), op=ALU.mult),
               reads=[posf, invf], writes=[ang])
          rop_op(dve, lambda e: e.tensor_scalar(out=ki[:], in0=ang[:], scalar1=1.0 / TWO_PI, scalar2=None, op0=ALU.mult),
               reads=[ang], writes=[ki])
          rop_op(dve, lambda e: e.tensor_copy(out=kf[:], in_=ki[:]), reads=[ki], writes=[kf])
          rop_op(dve, lambda e: e.scalar_tensor_tensor(out=rr[:], in0=kf[:], scalar=-C1A, in1=ang[:], op0=ALU.mult, op1=ALU.add),
               reads=[kf, ang], writes=[rr])
          rop_op(dve, lambda e: e.scalar_tensor_tensor(out=rr[:], in0=kf[:], scalar=-C1B, in1=rr[:], op0=ALU.mult, op1=ALU.add),
               reads=[kf, rr], writes=[rr])
          rop_op(dve, lambda e: e.tensor_scalar(out=rc[:], in0=rr[:], scalar1=math.pi / 2, scalar2=None, op0=ALU.add),
               reads=[rr], writes=[rc])
          rop_op(dve, lambda e: e.tensor_scalar(out=mm[:], in0=rc[:], scalar1=math.pi, scalar2=-TWO_PI, op0=ALU.is_gt, op1=ALU.mult),
               reads=[rc], writes=[mm])
          rop_op(dve, lambda e: e.tensor_tensor(out=rc[:], in0=rc[:], in1=mm[:], op=ALU.add), reads=[rc, mm], writes=[rc])
          PI_C = 3.1415925
          rop_op(dve, lambda e: e.tensor_scalar(out=rr[:], in0=rr[:], scalar1=-PI_C, scalar2=PI_C, op0=ALU.max, op1=ALU.min),
               reads=[rr], writes=[rr])
          rop_op(dve, lambda e: e.tensor_scalar(out=rc[:], in0=rc[:], scalar1=-PI_C, scalar2=PI_C, op0=ALU.max, op1=ALU.min),
               reads=[rc], writes=[rc])
          rop_op(act, lambda e: e.activation(out=sin_t[:], in_=rr[:], func=AF.Sin), reads=[rr], writes=[sin_t])
          rop_op(act, lambda e: e.activation(out=cos_t[:], in_=rc[:], func=AF.Sin), reads=[rc], writes=[cos_t])
          T.wait_all(sp, [posf, ang, kf, ki, rr, rc, mm])
          for e_ in (pool, pe):
              T.wait_all(e_, [posf, ang, kf, ki, rr, rc, mm])
          T.wait_all(act, [posf, ang, kf, ki, rr, rc, mm])
          T.wait_all(dve, [posf, ang, kf, ki, rr, rc, mm])

    NWF = 1152; NWT = 292
    win = K.sb("win", [128, 8, NWF + NWT], BF16)
    wq = K.sb("wq", [128, 3, 384], BF16)
    wkv = K.sb("wkv", [128, 2, 512], BF16)
    wes = ExitStack()
    wstg = [Tl(wes.enter_context(nc.sbuf_tensor(f"sbw_wstg{i}", [128, NWF + NWT], F32)), f"wstg{i}") for i in range(2)]
    wsem = [K.dsem() for _ in range(2)]
    for k in (range(8) if "W" in stages else []):
        st = wstg[k % 2]
        T.dma(sp, wsem[k % 2], st[:], d["w_in"][k * 128:(k + 1) * 128, :], writes=[st])
        T.op(pool, lambda e, st=st, k=k: e.tensor_scalar(out=win[:, k, :], in0=st[:], scalar1=nmw[:, k:k + 1],
                                                         scalar2=None, op0=ALU.mult),
             reads=[st, nmw], writes=[win])
    for k in (range(3) if "W" in stages else []):
        st = wstg[k % 2]
        T.dma(sp, wsem[k % 2], st[:, 0:384], d["w_q"][k * 128:(k + 1) * 128, :], writes=[st])
        T.op(pool, lambda e, st=st, k=k: e.tensor_scalar(out=wq[:, k, :], in0=st[:, 0:384], scalar1=qnw[:, k:k + 1],
                                                         scalar2=96.0 ** -0.5, op0=ALU.mult, op1=ALU.mult),
             reads=[st, qnw], writes=[wq])
    for k in (range(2) if "W" in stages else []):
        st = wstg[(k + 1) % 2]
        T.dma(sp, wsem[(k + 1) % 2], st[:, 0:512], d["w_kv"][k * 128:(k + 1) * 128, :], writes=[st])
        T.op(pool, lambda e, st=st, k=k: e.tensor_scalar(out=wkv[:, k, :], in0=st[:, 0:512], scalar1=kvnw[:, k:k + 1],
                                                         scalar2=None, op0=ALU.mult),
             reads=[st, kvnw], writes=[wkv])

    for e_ in T.engs:
        T.wait_all(e_, wstg)
    wes.close()
    KnT = K.sb("KnT", [128, 2, S], BF16)
    KrT = K.sb("KrT", [32, S], BF16)
    kt_bufs = [Buf(f"kt{t}") for t in range(NT)]
    Vst = K.sb("Vst", [128, NT, 4, 65], BF16)
    v_bufs = [Buf(f"v{t}") for t in range(NT)]
    if "v" not in stages:
        T.op(pool, lambda e: e.memset(Vst[:].rearrange("p t h c -> p (t h c)"), 1.0), writes=v_bufs)
    state = K.sb("state", [128, 256]); state_b = K.sb("state_b", [128, 256], BF16)
    if "s" not in stages:
        T.op(pool, lambda e: e.memset(state[:], 0.0), writes=[state])
        T.op(pool, lambda e: e.memset(state_b[:], 0.0), writes=[state_b])
    upre = K.sb("upre", [128, 4, 3 + 512])
    if "u" not in stages:
        T.op(pool, lambda e: e.memset(upre[:], 0.0), writes=[upre])

    NXB = 2
    xt = [K.sb(f"xt{i}", [128, 1024]) for i in range(NXB)]
    xsem = [K.dsem() for _ in range(NXB)]
    xn = [K.sb(f"xn{i}", [128, 1024], BF16) for i in range(2)]
    junk = K.sb("junk", [128, 1024], BF16)
    hT = [K.sb("hT0", [128, 8, 512], BF16)] * 2
    ssx = K.sb("ssx", [128, 4]); rsx = K.sb("rsx", [128, 4])
    qlT = K.sb("qlT", [128, 3, 512], BF16); sqT = K.sb("sqT", [128, 3, 512], BF16)
    kvT = K.sb("kvT", [128, 2, 512], BF16); sqkT = K.sb("sqkT", [128, 2, 512], BF16)
    cacc = [K.sb(f"cacc{i}", [128, 512]) for i in range(2)]
    cex = [K.sb("cex0", [128, 512])] * 2
    xsT = K.sb("xsT", [128, 2, 512]); BT = K.sb("BT", [128, 512], BF16); CT = K.sb("CT", [128, 512], BF16)
    zt = [K.sb(f"zt{i}", [128, 256]) for i in range(2)] * 2
    ez = [K.sb(f"ez{i}", [128, 256]) for i in range(2)] * 2
    dtr = K.sb("dtr", [128, 4, 4]); dtv = K.sb("dtv", [128, 4, 4]); adt = K.sb("adt", [128, 4, 4])
    krw = K.sb("krw", [128, 4, 32])
    QnT = K.sb("QnT", [128, 2, 512], BF16); QrT = K.sb("QrT", [32, 4, 512], BF16)
    qn = K.sb("qn", [128, 256], BF16); kn = K.sb("kn", [128, 256], BF16); qkr = K.sb("qkr", [128, 5, 32], BF16)
    rop = K.sb("rop", [128, 5, 32]); ropo = K.sb("ropo", [128, 5, 32]); rt1 = K.sb("rt1", [128, 5, 16]); rt2 = K.sb("rt2", [128, 5, 16])
    ssqk = K.sb("ssqk", [128, 2]); rsqk = K.sb("rsqk", [128, 2])
    PT = [K.sb(f"PT{i}", [128, 512], BF16) for i in range(2)]
    den = K.sb("recden", [128, 512]); rec = den
    mstage = [K.sb(f"mstage{i}", [64, 512], BF16) for i in range(2)]
    msem = [K.dsem() for _ in range(2)]
    ystage = [K.sb("ystage0", [128, 2, 512], BF16)] * 2
    ysem = [K.dsem()] * 2
    Lm = K.sb("Lm", [128, 4, 128]); Dm = Lm; MT = K.sb("MT", [128, 4, 128], BF16)
    CBm = K.sb("CBm", [128, 128]); ex12 = K.sb("ex12", [128, 12]); dtw = K.sb("dtw", [128, 4])
    xtok = K.sb("xtok", [128, 256]); Btok = K.sb("Btok", [128, 128], BF16)
    xdt = K.sb("xdt", [128, 256], BF16); xw = K.sb("xw", [128, 256], BF16)
    yt1 = K.sb("yt1", [128, 256]); yt2 = K.sb("yt2", [128, 256]); gz = yt2
    ysq = K.sb("ysq", [128, 256], BF16); ssy = K.sb("ssy", [128, 1]); rsy = K.sb("rsy", [128, 1])
    ynb = K.sb("ynb", [128, 256], BF16); stmp = K.sb("stmp", [128, 256])

    pbk = [K.psum(f"pb{i}", [128, 512]) for i in range(8)]
    rot = {"A": [0, 1], "B": [2, 3], "S": [4, 5], "O": [6], "D": [7]}
    rpos = {k: 0 for k in rot}

    def bank(cls):
        i = rot[cls][rpos[cls] % len(rot[cls])]
        rpos[cls] += 1
        return pbk[i]

    def bfv(pb):
        return pb.t[:].bitcast(BF16)

    def load_tile(t):
        if t < NT:
            b = t % NXB
            T.dma(sp, xsem[b], xt[b][:], d["x"][t * 128:(t + 1) * 128, :], writes=[xt[b]])

    for t_ in range(NXB):
        load_tile(t_)
    zsem = K.dsem()
    zt0 = K.sb("zhalo", [128, 4, 128], BF16)
    T.op(pool, lambda e: e.memset(zt0[:], 0.0), writes=[zt0])
    T.dma(pool, zsem, yx[0, :, 0:128].rearrange("(i p) c -> p i c", p=128), zt0[:], reads=[zt0], writes=[])

    for s in (range(NS) if "A" in stages else []):
        hTs = hT[s % 2]
        T.op(pool, lambda e: e.memset(ssx[:], 0.0), writes=[ssx])
        for j in range(4):
            b = (4 * s + j) % NXB
            T.op(act, lambda e, b=b, j=j: e.activation(out=junk[:], in_=xt[b][:], func=AF.Square, accum_out=ssx[:, j:j + 1]),
                 reads=[xt[b]], writes=[ssx])
            T.op(act, lambda e, j=j: e.activation(out=rsx[:, j:j + 1], in_=ssx[:, j:j + 1], func=AF.Ln, scale=1.0 / 1024, bias=EPS), reads=[ssx], writes=[rsx])
            T.op(act, lambda e, j=j: e.activation(out=rsx[:, j:j + 1], in_=rsx[:, j:j + 1], func=AF.Exp, scale=-0.5), reads=[rsx], writes=[rsx])
            xnj = xn[j % 2]
            T.op(pool, lambda e, b=b, j=j, xnj=xnj: e.tensor_scalar(out=xnj[:], in0=xt[b][:], scalar1=rsx[:, j:j + 1], scalar2=None, op0=ALU.mult),
                 reads=[xt[b], rsx], writes=[xnj])
            load_tile(4 * s + j + NXB)
            pb = bank("B")
            for k in range(8):
                T.op(pe, lambda e, pb=pb, k=k, xnj=xnj: e.transpose(out=bfv(pb)[:, k * 128:(k + 1) * 128], in_=xnj[:, k * 128:(k + 1) * 128], identity=ident_b[:]),
                     reads=[xnj, ident_b], writes=[pb])
            T.op(dve, lambda e, pb=pb, j=j: e.tensor_copy(out=hTs[:, :, j * 128:(j + 1) * 128], in_=bfv(pb).rearrange("p (k c) -> p k c", k=8)),
                 reads=[pb], writes=[hTs])

        for m in (range(9) if "B" in stages else []):
            pb = bank("A")
            for k in range(8):
                T.op(pe, lambda e, pb=pb, k=k, m=m: e.matmul(pb[:, :], lhsT=win[:, k, m * 128:(m + 1) * 128], rhs=hTs[:, k, :], start=(k == 0), stop=(k == 7)),
                     reads=[win, hTs], writes=[pb])
            if m < 4:
                T.op(dve, lambda e, pb=pb, m=m: e.tensor_copy(out=upre[:, m, 3:515], in_=pb[:, :]), reads=[pb], writes=[upre])
            elif m < 7:
                T.op(dve, lambda e, pb=pb, m=m: e.tensor_copy(out=qlT[:, m - 4, :], in_=pb[:, :]), reads=[pb], writes=[qlT])
                T.op(act, lambda e, pb=pb, m=m: e.activation(out=sqT[:, m - 4, :], in_=pb[:, :], func=AF.Square), reads=[pb], writes=[sqT])
            else:
                T.op(dve, lambda e, pb=pb, m=m: e.tensor_copy(out=kvT[:, m - 7, :], in_=pb[:, :]), reads=[pb], writes=[kvT])
                T.op(act, lambda e, pb=pb, m=m: e.activation(out=sqkT[:, m - 7, :], in_=pb[:, :], func=AF.Square), reads=[pb], writes=[sqkT])

        for m in (range(4) if "D" in stages else []):
            acc = cacc[m % 2]; ex = cex[m % 2]
            T.op(dve, lambda e, m=m, acc=acc: e.tensor_scalar(out=acc[:], in0=upre[:, m, 3:515], scalar1=cw[:, m, 3:4], scalar2=cb[:, m:m + 1], op0=ALU.mult, op1=ALU.add),
                 reads=[upre, cw, cb], writes=[acc])
            for tap in (2, 1, 0):
                T.op(dve, lambda e, m=m, acc=acc, tap=tap: e.scalar_tensor_tensor(out=acc[:], in0=upre[:, m, tap:tap + 512], scalar=cw[:, m, tap:tap + 1], in1=acc[:], op0=ALU.mult, op1=ALU.add),
                     reads=[upre, cw, acc], writes=[acc])
            T.op(act, lambda e, acc=acc, ex=ex: e.activation(out=ex[:], in_=acc[:], func=AF.Exp, scale=-1.0), reads=[acc], writes=[ex])
            T.op(pool, lambda e, ex=ex: e.tensor_scalar(out=ex[:], in0=ex[:], scalar1=1.0, scalar2=None, op0=ALU.add), reads=[ex], writes=[ex])
            T.op(dve, lambda e, ex=ex: e.reciprocal(out=ex[:], in_=ex[:]), reads=[ex], writes=[ex])
            if m < 2:
                T.op(pool, lambda e, m=m, acc=acc, ex=ex: e.tensor_tensor(out=xsT[:, m, :], in0=acc[:], in1=ex[:], op=ALU.mult), reads=[acc, ex], writes=[xsT])
            elif m == 2:
                T.op(pool, lambda e, acc=acc, ex=ex: e.tensor_tensor(out=BT[:], in0=acc[:], in1=ex[:], op=ALU.mult), reads=[acc, ex], writes=[BT])
            else:
                T.op(pool, lambda e, acc=acc, ex=ex: e.tensor_tensor(out=CT[:], in0=acc[:], in1=ex[:], op=ALU.mult), reads=[acc, ex], writes=[CT])
        T.op(pool, lambda e: e.tensor_copy(out=upre[:, :, 0:3], in_=upre[:, :, 512:515]), reads=[upre], writes=[upre])

        ys = ystage[s % 2]
        for j in (range(4) if "E" in stages else []):
            t = 4 * s + j
            js = slice(j * 128, (j + 1) * 128)
            pbc = bank("B")
            for k in range(8):
                T.op(pe, lambda e, pbc=pbc, k=k, j=j: e.matmul(pbc[:, 0:NWT], lhsT=hTs[:, k, j * 128:(j + 1) * 128], rhs=win[:, k, NWF:NWF + NWT], start=(k == 0), stop=(k == 7)),
                     reads=[win, hTs], writes=[pbc])
            T.op(dve, lambda e, pbc=pbc, j=j: e.tensor_copy(out=zt[j][:], in_=pbc[:, 0:256]), reads=[pbc], writes=[zt[j]])
            T.op(act, lambda e, pbc=pbc, j=j: e.activation(out=ez[j][:], in_=pbc[:, 0:256], func=AF.Exp, scale=-1.0), reads=[pbc], writes=[ez[j]])
            T.op(dve, lambda e, pbc=pbc, j=j: e.tensor_tensor(out=dtr[:, j, :], in0=pbc[:, 256:260], in1=dtb[:], op=ALU.add), reads=[pbc, dtb], writes=[dtr])
            T.op(dve, lambda e, pbc=pbc, j=j: e.tensor_copy(out=krw[:, j, :], in_=pbc[:, 260:292]), reads=[pbc], writes=[krw])
            T.op(act, lambda e, j=j: e.activation(out=dtv[:, j, :], in_=dtr[:, j, :], func=AF.Exp), reads=[dtr], writes=[dtv])
            T.op(act, lambda e, j=j: e.activation(out=dtv[:, j, :], in_=dtv[:, j, :], func=AF.Ln, bias=1.0), reads=[dtv], writes=[dtv])
            T.op(dve, lambda e, j=j: e.tensor_tensor(out=adt[:, j, :], in0=dtv[:, j, :], in1=aneg[:], op=ALU.mult), reads=[dtv, aneg], writes=[adt])
            pq = bank("B")
            for i in range(3):
                T.op(pe, lambda e, pq=pq, i=i, js=js: e.matmul(pq[:, 0:384], lhsT=qlT[:, i, js], rhs=wq[:, i, :], start=(i == 0), stop=(i == 2)),
                     reads=[qlT, wq], writes=[pq])
            for i in range(3):
                T.op(pe, lambda e, pq=pq, i=i, js=js: e.matmul(pq[:, 400:401], lhsT=sqT[:, i, js], rhs=ones_b[:, 0:1], start=(i == 0), stop=(i == 2)),
                     reads=[sqT, ones_b], writes=[pq])
            for i in range(2):
                T.op(pe, lambda e, pq=pq, i=i, js=js: e.matmul(pq[:, 401:402], lhsT=sqkT[:, i, js], rhs=ones_b[:, 0:1], start=(i == 0), stop=(i == 1)),
                     reads=[sqkT, ones_b], writes=[pq])
            pk = bank("B")
            for i in range(2):
                T.op(pe, lambda e, pk=pk, i=i, js=js: e.matmul(pk[:, :], lhsT=kvT[:, i, js], rhs=wkv[:, i, :], start=(i == 0), stop=(i == 1)),
                     reads=[kvT, wkv], writes=[pk])
            T.op(act, lambda e, pq=pq: e.activation(out=ssqk[:, 0:1], in_=pq[:, 400:401], func=AF.Ln, scale=1.0 / 384, bias=EPS), reads=[pq], writes=[ssqk])
            T.op(act, lambda e, pq=pq: e.activation(out=ssqk[:, 1:2], in_=pq[:, 401:402], func=AF.Ln, scale=1.0 / 256, bias=EPS), reads=[pq], writes=[ssqk])
            T.op(act, lambda e: e.activation(out=rsqk[:], in_=ssqk[:], func=AF.Exp, scale=-0.5), reads=[ssqk], writes=[rsqk])
            pq3 = pq[:, 0:384].rearrange("p (h c) -> p h c", h=4)
            T.op(dve, lambda e, pq3=pq3: e.tensor_scalar(out=qn[:].rearrange("p (h c) -> p h c", h=4), in0=pq3[:, :, 0:64], scalar1=rsqk[:, 0:1], scalar2=None, op0=ALU.mult),
                 reads=[pq, rsqk], writes=[qn])
            T.op(dve, lambda e, pq3=pq3: e.tensor_scalar(out=rop[:, 0:4, :], in0=pq3[:, :, 64:96], scalar1=rsqk[:, 0:1], scalar2=None, op0=ALU.mult),
                 reads=[pq, rsqk], writes=[rop])
            T.op(pool, lambda e, j=j: e.tensor_copy(out=rop[:, 4, :], in_=krw[:, j, :]), reads=[krw], writes=[rop])
            T.op(dve, lambda e, pk=pk: e.tensor_scalar(out=kn[:], in0=pk[:, 0:256], scalar1=rsqk[:, 1:2], scalar2=None, op0=ALU.mult),
                 reads=[pk, rsqk], writes=[kn])
            T.op(dve, lambda e, pk=pk, t=t: e.tensor_scalar(out=Vst[:, t, :, 0:64], in0=pk[:, 256:512].rearrange("p (h c) -> p h c", h=4), scalar1=rsqk[:, 1:2], scalar2=None, op0=ALU.mult),
                 reads=[pk, rsqk], writes=[v_bufs[t]])
            cosb = bc(cos_t[:, t, :], 1, [128, 5, 16]); sinb = bc(sin_t[:, t, :], 1, [128, 5, 16])
            T.op(pool, lambda e, cosb=cosb: e.tensor_tensor(out=rt1[:], in0=rop[:, :, 0:16], in1=cosb, op=ALU.mult), reads=[rop, cos_t], writes=[rt1])
            T.op(pool, lambda e, sinb=sinb: e.tensor_tensor(out=rt2[:], in0=rop[:, :, 16:32], in1=sinb, op=ALU.mult), reads=[rop, sin_t], writes=[rt2])
            T.op(pool, lambda e: e.tensor_tensor(out=qkr[:, :, 0:16], in0=rt1[:], in1=rt2[:], op=ALU.subtract), reads=[rt1, rt2], writes=[qkr])
            T.op(pool, lambda e, sinb=sinb: e.tensor_tensor(out=rt1[:], in0=rop[:, :, 0:16], in1=sinb, op=ALU.mult), reads=[rop, sin_t], writes=[rt1])
            T.op(pool, lambda e, cosb=cosb: e.tensor_tensor(out=rt2[:], in0=rop[:, :, 16:32], in1=cosb, op=ALU.mult), reads=[rop, cos_t], writes=[rt2])
            T.op(pool, lambda e: e.tensor_tensor(out=qkr[:, :, 16:32], in0=rt1[:], in1=rt2[:], op=ALU.add), reads=[rt1, rt2], writes=[qkr])
            pt = bank("B")
            for hp in range(2):
                T.op(pe, lambda e, pt=pt, hp=hp: e.transpose(out=bfv(pt)[:, hp * 128:(hp + 1) * 128], in_=qn[:, hp * 128:(hp + 1) * 128], identity=ident_b[:]),
                     reads=[qn, ident_b], writes=[pt])
            for hp in range(2):
                T.op(pe, lambda e, pt=pt, hp=hp: e.transpose(out=bfv(pt)[:, 256 + hp * 128:256 + (hp + 1) * 128], in_=kn[:, hp * 128:(hp + 1) * 128], identity=ident_b[:]),
                     reads=[kn, ident_b], writes=[pt])
            pt2 = bank("B")
            for h in range(5):
                T.op(pe, lambda e, pt2=pt2, h=h: e.transpose(out=bfv(pt2)[0:32, h * 128:(h + 1) * 128], in_=qkr[:, h, :], identity=ident_b[:]),
                     reads=[qkr, ident_b], writes=[pt2])
            T.op(dve, lambda e, pt=pt, js=js: e.tensor_copy(out=QnT[:, :, js], in_=bfv(pt)[:, 0:256].rearrange("p (h c) -> p h c", h=2)),
                 reads=[pt], writes=[QnT])
            T.op(dve, lambda e, pt=pt, t=t: e.tensor_copy(out=KnT[:, :, t * 128:(t + 1) * 128], in_=bfv(pt)[:, 256:512].rearrange("p (h c) -> p h c", h=2)),
                 reads=[pt], writes=[kt_bufs[t]])
            T.op(act, lambda e, pt2=pt2, js=js: e.activation(out=QrT[0:32, :, js], in_=bfv(pt2)[0:32, 0:512].rearrange("p (h c) -> p h c", h=4), func=AF.Copy),
                 reads=[pt2], writes=[QrT])
            T.op(act, lambda e, pt2=pt2, t=t: e.activation(out=KrT[0:32, t * 128:(t + 1) * 128], in_=bfv(pt2)[0:32, 512:640], func=AF.Copy),
                 reads=[pt2], writes=[kt_bufs[t]])

            pd = bank("D")
            T.op(pe, lambda e, pd=pd, j=j: e.matmul(pd[:, 0:4], lhsT=tri_le[:], rhs=adt[:, j, :], start=True, stop=True), reads=[tri_le, adt], writes=[pd])
            T.op(pe, lambda e, pd=pd, j=j: e.matmul(pd[:, 4:8], lhsT=tri_gt[:], rhs=adt[:, j, :], start=True, stop=True), reads=[tri_gt, adt], writes=[pd])
            T.op(pe, lambda e, pd=pd, j=j: e.matmul(pd[:, 8:12], lhsT=ones_f[:], rhs=adt[:, j, :], start=True, stop=True), reads=[ones_f, adt], writes=[pd])
            T.op(act, lambda e, pd=pd: e.activation(out=ex12[:], in_=pd[:, 0:12], func=AF.Exp), reads=[pd], writes=[ex12])
            T.op(dve, lambda e, j=j: e.tensor_tensor(out=Lm[:], in0=bc(tri_gt[:], 1, [128, 4, 128]), in1=bc(adt[:, j, :], 2, [128, 4, 128]), op=ALU.mult),
                 reads=[tri_gt, adt], writes=[Lm])
            pg = bank("S")
            for h in range(4):
                T.op(pe, lambda e, pg=pg, h=h: e.matmul(pg[:, h * 128:(h + 1) * 128], lhsT=Lm[:, h, :], rhs=tri_le[:], start=True, stop=True),
                     reads=[Lm, tri_le], writes=[pg])
            T.op(act, lambda e, pg=pg: e.activation(out=Dm[:].rearrange("p h c -> p (h c)"), in_=pg[:, :], func=AF.Exp), reads=[pg], writes=[Dm])
            pc = bank("D")
            T.op(pe, lambda e, pc=pc, js=js: e.matmul(pc[:, 0:128], lhsT=BT[:, js], rhs=CT[:, js], start=True, stop=True), reads=[BT, CT], writes=[pc])
            T.op(dve, lambda e, pc=pc: e.tensor_tensor(out=CBm[:], in0=pc[:, 0:128], in1=tri_le[:], op=ALU.mult), reads=[pc, tri_le], writes=[CBm])
            T.op(dve, lambda e: e.tensor_tensor(out=MT[:], in0=Dm[:], in1=bc(CBm[:], 1, [128, 4, 128]), op=ALU.mult), reads=[Dm, CBm], writes=[MT])
            px = bank("B")
            for i in range(2):
                T.op(pe, lambda e, px=px, i=i, js=js: e.transpose(out=px[:, i * 128:(i + 1) * 128], in_=xsT[:, i, js], identity=ident_f[:]),
                     reads=[xsT, ident_f], writes=[px])
            pbt = bank("B")
            T.op(pe, lambda e, pbt=pbt, js=js: e.transpose(out=bfv(pbt)[:, 0:128], in_=BT[:, js], identity=ident_b[:]), reads=[BT, ident_b], writes=[pbt])
            T.op(dve, lambda e, px=px: e.tensor_copy(out=xtok[:], in_=px[:, 0:256]), reads=[px], writes=[xtok])
            T.op(dve, lambda e, pbt=pbt: e.tensor_copy(out=Btok[:], in_=bfv(pbt)[:, 0:128]), reads=[pbt], writes=[Btok])
            T.op(dve, lambda e: e.tensor_tensor(out=dtw[:], in0=dtv[:, j, :], in1=ex12[:, 4:8], op=ALU.mult), reads=[dtv, ex12], writes=[dtw])
            T.op(pool, lambda e, j=j: e.tensor_tensor(out=xdt[:].rearrange("p (h c) -> p h c", h=4), in0=xtok[:].rearrange("p (h c) -> p h c", h=4),
                                                      in1=bc(dtv[:, j, :], 2, [128, 4, 64]), op=ALU.mult), reads=[xtok, dtv], writes=[xdt])
            T.op(pool, lambda e: e.tensor_tensor(out=xw[:].rearrange("p (h c) -> p h c", h=4), in0=xtok[:].rearrange("p (h c) -> p h c", h=4),
                                                 in1=bc(dtw[:], 2, [128, 4, 64]), op=ALU.mult), reads=[xtok, dtw], writes=[xw])
            py = bank("A")
            for h in range(4):
                T.op(pe, lambda e, py=py, h=h: e.matmul(py[:, h * 64:(h + 1) * 64], lhsT=MT[:, h, :], rhs=xdt[:, h * 64:(h + 1) * 64], start=True, stop=True),
                     reads=[MT, xdt], writes=[py])
            T.op(pe, lambda e, py=py, js=js: e.matmul(py[:, 256:512], lhsT=CT[:, js], rhs=state_b[:], start=True, stop=True), reads=[CT, state_b], writes=[py])
            pst = bank("D")
            T.op(pe, lambda e, pst=pst: e.matmul(pst[:, 0:256], lhsT=Btok[:], rhs=xw[:], start=True, stop=True), reads=[Btok, xw], writes=[pst])
            T.op(dve, lambda e: e.tensor_tensor(out=stmp[:].rearrange("p (h c) -> p h c", h=4), in0=state[:].rearrange("p (h c) -> p h c", h=4),
                                                in1=bc(ex12[:, 8:12], 2, [128, 4, 64]), op=ALU.mult), reads=[state, ex12], writes=[stmp])
            T.op(dve, lambda e, pst=pst: e.tensor_tensor(out=state[:], in0=stmp[:], in1=pst[:, 0:256], op=ALU.add), reads=[stmp, pst], writes=[state])
            T.op(pool, lambda e: e.tensor_copy(out=state_b[:], in_=state[:]), reads=[state], writes=[state_b])
            T.op(dve, lambda e, py=py: e.tensor_tensor(out=yt1[:].rearrange("p (h c) -> p h c", h=4), in0=py[:, 256:512].rearrange("p (h c) -> p h c", h=4),
                                                       in1=bc(ex12[:, 0:4], 2, [128, 4, 64]), op=ALU.mult), reads=[py, ex12], writes=[yt1])
            T.op(dve, lambda e, py=py: e.tensor_tensor(out=yt1[:], in0=yt1[:], in1=py[:, 0:256], op=ALU.add), reads=[py, yt1], writes=[yt1])
            T.op(pool, lambda e: e.tensor_tensor(out=yt2[:].rearrange("p (h c) -> p h c", h=4), in0=xtok[:].rearrange("p (h c) -> p h c", h=4),
                                                 in1=bc(dsk[:], 2, [128, 4, 64]), op=ALU.mult), reads=[xtok, dsk], writes=[yt2])
            T.op(pool, lambda e: e.tensor_tensor(out=yt1[:], in0=yt1[:], in1=yt2[:], op=ALU.add), reads=[yt1, yt2], writes=[yt1])
            T.op(pool, lambda e, j=j: e.tensor_scalar(out=ez[j][:], in0=ez[j][:], scalar1=1.0, scalar2=None, op0=ALU.add), reads=[ez[j]], writes=[ez[j]])
            T.op(dve, lambda e, j=j: e.reciprocal(out=ez[j][:], in_=ez[j][:]), reads=[ez[j]], writes=[ez[j]])
            T.op(pool, lambda e, j=j: e.tensor_tensor(out=gz[:], in0=zt[j][:], in1=ez[j][:], op=ALU.mult), reads=[zt[j], ez[j]], writes=[gz])
            T.op(pool, lambda e: e.tensor_tensor(out=yt1[:], in0=yt1[:], in1=gz[:], op=ALU.mult), reads=[yt1, gz], writes=[yt1])
            T.op(pool, lambda e: e.memset(ssy[:], 0.0), writes=[ssy])
            T.op(act, lambda e: e.activation(out=ysq[:], in_=yt1[:], func=AF.Square, accum_out=ssy[:]), reads=[yt1], writes=[ssy])
            T.op(act, lambda e: e.activation(out=rsy[:], in_=ssy[:], func=AF.Ln, scale=1.0 / 256, bias=EPS), reads=[ssy], writes=[rsy])
            T.op(act, lambda e: e.activation(out=rsy[:], in_=rsy[:], func=AF.Exp, scale=-0.5), reads=[rsy], writes=[rsy])
            T.op(dve, lambda e: e.scalar_tensor_tensor(out=ynb[:], in0=yt1[:], scalar=rsy[:, 0:1], in1=snw[:], op0=ALU.mult, op1=ALU.mult),
                 reads=[yt1, rsy, snw], writes=[ynb])
            pyt = bank("B")
            for i in range(2):
                T.op(pe, lambda e, pyt=pyt, i=i: e.transpose(out=bfv(pyt)[:, i * 128:(i + 1) * 128], in_=ynb[:, i * 128:(i + 1) * 128], identity=ident_b[:]),
                     reads=[ynb, ident_b], writes=[pyt])
            T.op(dve, lambda e, pyt=pyt, js=js: e.tensor_copy(out=ys[:, :, js], in_=bfv(pyt)[:, 0:256].rearrange("p (i c) -> p i c", i=2)),
                 reads=[pyt], writes=[ys])
        r_dst = (s * 512) // SQ
        c_dst = 128 + (s * 512) % SQ
        halo_dst = r_dst + 1 if ((s * 512) % SQ == SQ - 512 and r_dst < 3) else None
        if "O" in stages:
          T.dma(pool, ysem[s % 2], yx[r_dst, 0:256, c_dst:c_dst + 512].rearrange("(i p) c -> p i c", p=128), ys[:], reads=[ys], writes=[])
          if halo_dst is not None:
              T.dma(pool, ysem[s % 2], yx[halo_dst, 0:256, 0:128].rearrange("(i p) c -> p i c", p=128), ys[:, :, 384:512], reads=[ys], writes=[])

        pti = 0
        for h in (range(4) if "G" in stages else []):
            po = bank("O")
            nkt = 4 * (s + 1)
            for kt in range(nkt):
                jd = kt - 4 * s
                q0 = 128 * jd if jd > 0 else 0
                pS = bank("S")
                pb0 = (h % 2) * 64
                T.op(pe, lambda e, pS=pS, kt=kt, q0=q0, h=h, pb0=pb0: e.matmul(pS[:, q0:512], lhsT=KnT[pb0:pb0 + 64, h // 2, kt * 128:(kt + 1) * 128], rhs=QnT[pb0:pb0 + 64, h // 2, q0:512], start=True, stop=False),
                     reads=[kt_bufs[kt], QnT], writes=[pS])
                T.op(pe, lambda e, pS=pS, kt=kt, q0=q0, h=h: e.matmul(pS[:, q0:512], lhsT=KrT[0:32, kt * 128:(kt + 1) * 128], rhs=QrT[0:32, h, q0:512], start=False, stop=True),
                     reads=[kt_bufs[kt], QrT], writes=[pS])
                ptile = PT[pti % 2]; pti += 1
                T.op(act, lambda e, pS=pS, q0=q0, ptile=ptile: e.activation(out=ptile[:, q0:512], in_=pS[:, q0:512], func=AF.Exp), reads=[pS], writes=[ptile])
                if jd >= 0:
                    T.op(pool, lambda e, q0=q0, ptile=ptile: e.memset(ptile[64:128, q0:q0 + 64], 0.0), writes=[ptile])
                T.op(pe, lambda e, po=po, kt=kt, q0=q0, h=h, ptile=ptile, nkt=nkt: e.matmul(po[0:65, q0:512], lhsT=Vst[:, kt, h, :], rhs=ptile[:, q0:512], start=(kt == 0), stop=(kt == nkt - 1)),
                     reads=[v_bufs[kt], ptile], writes=[po])
            T.op(act, lambda e, po=po: e.activation(out=den[64:65, :], in_=po[64:65, :], func=AF.Copy), reads=[po], writes=[den])
            pb2 = bank("D")
            T.op(pe, lambda e, pb2=pb2: e.matmul(pb2[0:64, :], lhsT=ones_f[64:65, 0:64], rhs=den[64:65, :], start=True, stop=True), reads=[ones_f, den], writes=[pb2])
            T.op(dve, lambda e, pb2=pb2: e.reciprocal(out=rec[0:64, :], in_=pb2[0:64, :]), reads=[pb2], writes=[rec])
            ms = mstage[h % 2]
            T.op(dve, lambda e, po=po, ms=ms: e.tensor_tensor(out=ms[:], in0=po[0:64, :], in1=rec[0:64, :], op=ALU.mult), reads=[po, rec], writes=[ms])
            if "O" in stages:
                T.dma(pool, msem[h % 2], yx[r_dst, 256 + h * 64:256 + (h + 1) * 64, c_dst:c_dst + 512], ms[:], reads=[ms], writes=[])
                if halo_dst is not None:
                    T.dma(pool, msem[h % 2], yx[halo_dst, 256 + h * 64:256 + (h + 1) * 64, 0:128], ms[:, 384:512], reads=[ms], writes=[])
    return ysem + msem + [zsem]


def phase2(K, S, d, yr, out, gathered=False, rank_ap=None):
    nc, T = K.nc, K.T
    pe, act, dve, pool, sp = T.pe, T.act, T.dve, T.pool, T.sp
    TQ = S // 4
    NTL = TQ // 128 + 1
    NG = TQ // 512
    pbk = [K.psum(f"qb{i}", [128, 512]) for i in range(8)]
    rot = {"A": [0, 1, 2, 3], "B": [4, 5], "C": [6, 7]}
    rpos = {k: 0 for k in rot}

    def bank(cls):
        i = rot[cls][rpos[cls] % len(rot[cls])]
        rpos[cls] += 1
        return pbk[i]

    def bfv(pb):
        return pb.t[:].bitcast(BF16)

    ident_b = K.sb("p2_ident_b", [128, 128], BF16); ident_f = K.sb("p2_ident_f", [128, 128])
    ones_b = K.sb("p2_ones_b", [128, 128], BF16)
    hflag = K.sb("p2_hflag", [128, 1])
    fcw = K.sb("p2_fcw", [128, 44, 3]); fcb = K.sb("p2_fcb", [128, 44])
    cs = K.dsem()
    T.dma(sp, cs, ident_f[:], d["c_ident"][:, :], writes=[ident_f])
    T.dma(sp, cs, hflag[:], d["hflag"][:, :], writes=[hflag])
    T.dma(sp, cs, fcw[:], d["fcw"][:, :, :], writes=[fcw])
    T.dma(sp, cs, fcb[:], d["fcb"][:, :], writes=[fcb])
    for tl_ in (ident_f, hflag, fcw, fcb):
        tl_.buf.w = (cs, cs.count)
    T.op(dve, lambda e: e.tensor_copy(out=ident_b[:], in_=ident_f[:]), reads=[ident_f], writes=[ident_b])
    T.op(pool, lambda e: e.memset(ones_b[:], 1.0), writes=[ones_b])

    xres = K.sb("xres", [128, NTL, 1024])
    xres_b = [Buf(f"xres{i}") for i in range(NTL)]
    ssn = K.sb("p2_ssn", [128, 4]); rsn = K.sb("p2_rsn", [128, 4])
    junk = K.sb("p2_junk", [128, 1024], BF16)
    xnb = [K.sb(f"p2_xnb{i}", [128, 1024], BF16) for i in range(2)]
    wsems = []

    def wload(dst_tl, dst_ap, src_ap):
        sem = K.dsem(); wsems.append(sem)
        T.dma(pool, sem, dst_ap, src_ap, writes=[dst_tl])

    def norm_transpose(tiles, nw, gain_idx, dstT):
        n = len(tiles)
        T.op(pool, lambda e: e.memset(ssn[:], 0.0), writes=[ssn])
        for j, (ap, b) in enumerate(tiles):
            T.op(act, lambda e, ap=ap, j=j: e.activation(out=junk[:], in_=ap, func=AF.Square, accum_out=ssn[:, j:j + 1]), reads=[b], writes=[ssn])
        T.op(act, lambda e: e.activation(out=rsn[:, 0:n], in_=ssn[:, 0:n], func=AF.Ln, scale=1.0 / 1024, bias=EPS), reads=[ssn], writes=[rsn])
        T.op(act, lambda e: e.activation(out=rsn[:, 0:n], in_=rsn[:, 0:n], func=AF.Exp, scale=-0.5), reads=[rsn], writes=[rsn])
        for j, (ap, b) in enumerate(tiles):
            xn_ = xnb[j % 2]
            T.op(dve, lambda e, ap=ap, j=j, xn_=xn_: e.scalar_tensor_tensor(out=xn_[:], in0=ap, scalar=rsn[:, j:j + 1], in1=nw[:, gain_idx, :], op0=ALU.mult, op1=ALU.mult),
                 reads=[b, rsn, nw], writes=[xn_])
            pb = bank("C")
            for k in range(8):
                T.op(pe, lambda e, pb=pb, k=k, xn_=xn_: e.transpose(out=bfv(pb)[:, k * 128:(k + 1) * 128], in_=xn_[:, k * 128:(k + 1) * 128], identity=ident_b[:]),
                     reads=[xn_, ident_b], writes=[pb])
            T.op(act, lambda e, pb=pb, j=j: e.activation(out=dstT[:, :, j * 128:(j + 1) * 128], in_=bfv(pb).rearrange("p (k c) -> p k c", k=8), func=AF.Copy),
                 reads=[pb], writes=[dstT])

    with ExitStack() as es1:
        sb1 = lambda name, shape, dt=F32: Tl(es1.enter_context(nc.sbuf_tensor("sb_" + name, list(shape), dt)), name)
        nwA = sb1("nwA", [128, 2, 1024])
        nsA = K.dsem()
        T.dma(sp, nsA, nwA[:], d["nw"][:, 0:2, :], writes=[nwA])
        wout = sb1("wout", [128, 16, 1024], BF16)
        wq = sb1("wmq", [128, 8, 1024], BF16); wo = sb1("wmo", [128, 8, 1024], BF16)
        KTm = sb1("KTm", [128, 8, 256], BF16); Vm = sb1("Vm", [128, 2, 1024], BF16)
        wload(wout, wout[:], d["w_out"].rearrange("(k p) n -> p k n", p=128))
        wload(wq, wq[:], d["w_mq"].rearrange("(k p) n -> p k n", p=128))
        wload(wo, wo[:], d["w_mo"].rearrange("(k p) n -> p k n", p=128))
        with ExitStack() as es0:
            sb0 = lambda name, shape, dt=F32: Tl(es0.enter_context(nc.sbuf_tensor("sb_" + name, list(shape), dt)), name)
            wk = sb0("wmk", [128, 8, 1024], BF16); wv = sb0("wmv", [128, 8, 1024], BF16)
            memt = sb0("memt", [128, 2, 1024]); memT = sb0("memT", [128, 8, 256], BF16)
            wload(wk, wk[:], d["w_mk"].rearrange("(k p) n -> p k n", p=128))
            wload(wv, wv[:], d["w_mv"].rearrange("(k p) n -> p k n", p=128))
            msem_ = K.dsem()
            T.dma(sp, msem_, memt[:], d["mem"].rearrange("(t p) n -> p t n", p=128), writes=[memt])
            norm_transpose([(memt[:, 0, :], memt), (memt[:, 1, :], memt)], nwA, 1, memT)
            for m in range(8):
                pb = bank("A")
                for k in range(8):
                    T.op(pe, lambda e, pb=pb, k=k, m=m: e.matmul(pb[:, 0:256], lhsT=wk[:, k, m * 128:(m + 1) * 128], rhs=memT[:, k, :], start=(k == 0), stop=(k == 7)),
                         reads=[wk, memT], writes=[pb])
                T.op(dve, lambda e, pb=pb, m=m: e.tensor_copy(out=KTm[:, m, :], in_=pb[:, 0:256]), reads=[pb], writes=[KTm])
            for mt in range(2):
                for half in range(2):
                    pb = bank("A")
                    for k in range(8):
                        T.op(pe, lambda e, pb=pb, k=k, mt=mt, half=half: e.matmul(pb[:, :], lhsT=memT[:, k, mt * 128:(mt + 1) * 128], rhs=wv[:, k, half * 512:(half + 1) * 512], start=(k == 0), stop=(k == 7)),
                             reads=[wv, memT], writes=[pb])
                    T.op(dve, lambda e, pb=pb, mt=mt, half=half: e.tensor_copy(out=Vm[:, mt, half * 512:(half + 1) * 512], in_=pb[:, :]), reads=[pb], writes=[Vm])
            for e_ in T.engs:
                T.wait_all(e_, [wk, wv, memt, memT])

        ybuf = [sb1("ybuf0", [128, 16, 512], BF16)] * 2
        ysem_ = [K.dsem()] * 2
        xin = [sb1(f"xin{i}", [128, 1024]) for i in range(2)]
        xsem_ = [K.dsem() for _ in range(2)]
        hmT = sb1("hmT", [128, 8, 512], BF16); qT = sb1("qT", [128, 8, 512], BF16); oT = hmT
        PTm = [sb1(f"PTm{i}", [128, 512], BF16) for i in range(2)]
        recm = sb1("recm", [128, 512])

        groups = [(0, 1)] + [(1 + 4 * g, 4) for g in range(NG)]
        xcnt = 0
        for gi, (t0, nt) in enumerate(groups):
            W = nt * 128
            c0 = t0 * 128
            yb = ybuf[gi % 2]
            for g4 in range(4):
                T.dma(sp, ysem_[gi % 2], yb[:, g4 * 4:(g4 + 1) * 4, 0:W], yr[g4, :, c0:c0 + W].rearrange("(kb p) c -> p kb c", p=128), writes=[yb])
            for j in range(nt):
                t = t0 + j
                xi = xin[xcnt % 2]
                T.dma(sp, xsem_[xcnt % 2], xi[:], d["xh"][t * 128:(t + 1) * 128, :], writes=[xi]); xcnt += 1
                for half in range(2):
                    pb = bank("A")
                    for k in range(16):
                        T.op(pe, lambda e, pb=pb, k=k, j=j, half=half: e.matmul(pb[:, :], lhsT=yb[:, k, j * 128:(j + 1) * 128], rhs=wout[:, k, half * 512:(half + 1) * 512], start=(k == 0), stop=(k == 15)),
                             reads=[yb, wout], writes=[pb])
                    T.op(dve, lambda e, pb=pb, t=t, half=half, xi=xi: e.tensor_tensor(out=xres[:, t, half * 512:(half + 1) * 512], in0=pb[:, :], in1=xi[:, half * 512:(half + 1) * 512], op=ALU.add),
                         reads=[pb, xi], writes=[xres_b[t]])
            norm_transpose([(xres[:, t0 + j, :], xres_b[t0 + j]) for j in range(nt)], nwA, 0, hmT)
            for m in range(8):
                pb = bank("A")
                for k in range(8):
                    T.op(pe, lambda e, pb=pb, k=k, m=m: e.matmul(pb[:, 0:W], lhsT=wq[:, k, m * 128:(m + 1) * 128], rhs=hmT[:, k, 0:W], start=(k == 0), stop=(k == 7)),
                         reads=[wq, hmT], writes=[pb])
                T.op(act, lambda e, pb=pb, m=m: e.activation(out=qT[:, m, 0:W], in_=pb[:, 0:W], func=AF.Copy, scale=256.0 ** -0.5), reads=[pb], writes=[qT])
            for h in range(4):
                for mt in range(2):
                    pb = bank("B")
                    for i in range(2):
                        T.op(pe, lambda e, pb=pb, i=i, mt=mt, h=h: e.matmul(pb[:, 0:W], lhsT=KTm[:, 2 * h + i, mt * 128:(mt + 1) * 128], rhs=qT[:, 2 * h + i, 0:W], start=(i == 0), stop=(i == 1)),
                             reads=[KTm, qT], writes=[pb])
                    T.op(act, lambda e, pb=pb, mt=mt: e.activation(out=PTm[mt][:, 0:W], in_=pb[:, 0:W], func=AF.Exp), reads=[pb], writes=[PTm[mt]])
                pdn = bank("B")
                for mt in range(2):
                    T.op(pe, lambda e, pdn=pdn, mt=mt: e.matmul(pdn[:, 0:W], lhsT=ones_b[:], rhs=PTm[mt][:, 0:W], start=(mt == 0), stop=(mt == 1)), reads=[ones_b, PTm[mt]], writes=[pdn])
                T.op(dve, lambda e, pdn=pdn: e.reciprocal(out=recm[:, 0:W], in_=pdn[:, 0:W]), reads=[pdn], writes=[recm])
                for i in range(2):
                    pb = bank("A")
                    for mt in range(2):
                        T.op(pe, lambda e, pb=pb, i=i, mt=mt, h=h: e.matmul(pb[:, 0:W], lhsT=Vm[:, mt, h * 256 + i * 128:h * 256 + (i + 1) * 128], rhs=PTm[mt][:, 0:W], start=(mt == 0), stop=(mt == 1)),
                             reads=[Vm, PTm[mt]], writes=[pb])
                    T.op(dve, lambda e, pb=pb, i=i, h=h: e.tensor_tensor(out=oT[:, 2 * h + i, 0:W], in0=pb[:, 0:W], in1=recm[:, 0:W], op=ALU.mult), reads=[pb, recm], writes=[oT])
            for j in range(nt):
                t = t0 + j
                for half in range(2):
                    pb = bank("A")
                    for k in range(8):
                        T.op(pe, lambda e, pb=pb, k=k, j=j, half=half: e.matmul(pb[:, :], lhsT=oT[:, k, j * 128:(j + 1) * 128], rhs=wo[:, k, half * 512:(half + 1) * 512], start=(k == 0), stop=(k == 7)),
                             reads=[oT, wo], writes=[pb])
                    T.op(dve, lambda e, pb=pb, t=t, half=half: e.tensor_tensor(out=xres[:, t, half * 512:(half + 1) * 512], in0=pb[:, :], in1=xres[:, t, half * 512:(half + 1) * 512], op=ALU.add),
                         reads=[pb, xres_b[t]], writes=[xres_b[t]])
        for e_ in T.engs:
            T.wait_all(e_, [nwA, wout, wq, wo, KTm, Vm, hmT, qT, recm] + ybuf + xin + PTm)

    with ExitStack() as es2:
        sb2 = lambda name, shape, dt=F32: Tl(es2.enter_context(nc.sbuf_tensor("sb_" + name, list(shape), dt)), name)
        nwC = sb2("nwC", [128, 2, 1024])
        nsC = K.dsem()
        T.dma(sp, nsC, nwC[:], d["nw"][:, 2:4, :], writes=[nwC])
        wdn = sb2("wdn", [128, 22, 1024], BF16)
        wload(wdn, wdn[:], d["w_dn"].rearrange("(k p) n -> p k n", p=128))
        hfT = sb2("hfT", [128, 8, 512], BF16); hfH = sb2("hfH", [128, 8, 128], BF16)
        actT = sb2("actT", [128, 22, 512], BF16)
        wub = [sb2(f"wub{i}", [128, 2, 8, 128], BF16) for i in range(3)]
        wusem = [K.dsem() for _ in range(3)]; wsems.extend(wusem)
        uh = sb2("uh", [128, 44, 2])
        ub = [sb2(f"ub{i}", [128, 514]) for i in range(2)]
        cg = [sb2(f"cg{i}", [128, 512]) for i in range(2)]; cv = [sb2(f"cv{i}", [128, 512]) for i in range(2)]
        sg = [sb2(f"sg{i}", [128, 512]) for i in range(2)]
        x3 = [sb2(f"x3_{i}", [128, 1024]) for i in range(2)]; junk2 = sb2("junk2", [128, 1024], BF16)
        ss3 = sb2("ss3", [128, 1]); rs3 = sb2("rs3", [128, 1])
        osem = [K.dsem() for _ in range(2)]
        norm_transpose([(xres[:, 0, :], xres_b[0])], nwC, 0, hfH)
        ocnt = 0
        wcnt = 0
        for g in range(NG):
            t0 = 1 + 4 * g
            norm_transpose([(xres[:, t0 + j, :], xres_b[t0 + j]) for j in range(4)], nwC, 0, hfT)
            for m in range(22):
                wb = wub[wcnt % 3]
                T.dma(pool, wusem[wcnt % 3], wb[:], d["w_up"][m, :, :, :, :], writes=[wb]); wcnt += 1
                res = []
                for which in range(2):
                    fm = m + 22 * which
                    if g == 0:
                        ph = bank("B")
                        for k in range(8):
                            T.op(pe, lambda e, ph=ph, k=k, which=which: e.matmul(ph[:, 0:2], lhsT=wb[:, which, k, :], rhs=hfH[:, k, 126:128], start=(k == 0), stop=(k == 7)),
                                 reads=[wb, hfH], writes=[ph])
                        T.op(dve, lambda e, ph=ph, fm=fm: e.tensor_scalar(out=uh[:, fm, :], in0=ph[:, 0:2], scalar1=hflag[:, 0:1], scalar2=None, op0=ALU.mult),
                             reads=[ph, hflag], writes=[uh])
                    pb = bank("A")
                    for k in range(8):
                        T.op(pe, lambda e, pb=pb, k=k, which=which: e.matmul(pb[:, :], lhsT=wb[:, which, k, :], rhs=hfT[:, k, :], start=(k == 0), stop=(k == 7)),
                             reads=[wb, hfT], writes=[pb])
                    u = ub[which]
                    T.op(pool, lambda e, u=u, fm=fm: e.tensor_copy(out=u[:, 0:2], in_=uh[:, fm, :]), reads=[uh], writes=[u])
                    T.op(act, lambda e, u=u, pb=pb: e.activation(out=u[:, 2:514], in_=pb[:, :], func=AF.Copy), reads=[pb], writes=[u])
                    acc = (cg if which == 0 else cv)[m % 2]
                    T.op(dve, lambda e, u=u, acc=acc, fm=fm: e.tensor_scalar(out=acc[:], in0=u[:, 2:514], scalar1=fcw[:, fm, 2:3], scalar2=fcb[:, fm:fm + 1], op0=ALU.mult, op1=ALU.add),
                         reads=[u, fcw, fcb], writes=[acc])
                    for tap in (1, 0):
                        T.op(dve, lambda e, u=u, acc=acc, fm=fm, tap=tap: e.scalar_tensor_tensor(out=acc[:], in0=u[:, tap:tap + 512], scalar=fcw[:, fm, tap:tap + 1], in1=acc[:], op0=ALU.mult, op1=ALU.add),
                             reads=[u, fcw, acc], writes=[acc])
                    T.op(pool, lambda e, u=u, fm=fm: e.tensor_copy(out=uh[:, fm, :], in_=u[:, 512:514]), reads=[u], writes=[uh])
                    res.append(acc)
                accg, accv = res
                sgm = sg[m % 2]
                T.op(act, lambda e: e.activation(out=sgm[:], in_=accg[:], func=AF.Exp, scale=-1.0), reads=[accg], writes=[sgm])
                T.op(act, lambda e: e.activation(out=sgm[:], in_=sgm[:], func=AF.Ln, bias=1.0), reads=[sgm], writes=[sgm])
                T.op(act, lambda e: e.activation(out=sgm[:], in_=sgm[:], func=AF.Exp, scale=-1.0), reads=[sgm], writes=[sgm])
                T.op(pool, lambda e: e.tensor_tensor(out=accg[:], in0=accg[:], in1=sgm[:], op=ALU.mult), reads=[accg, sgm], writes=[accg])
                T.op(pool, lambda e, m=m: e.tensor_tensor(out=actT[:, m, :], in0=accg[:], in1=accv[:], op=ALU.mult), reads=[accg, accv], writes=[actT])
            for j in range(4):
                t = t0 + j
                x3t = x3[ocnt % 2]; ott = x3t
                for half in range(2):
                    pb = bank("A")
                    for k in range(22):
                        T.op(pe, lambda e, pb=pb, k=k, j=j, half=half: e.matmul(pb[:, :], lhsT=actT[:, k, j * 128:(j + 1) * 128], rhs=wdn[:, k, half * 512:(half + 1) * 512], start=(k == 0), stop=(k == 21)),
                             reads=[actT, wdn], writes=[pb])
                    T.op(dve, lambda e, pb=pb, t=t, half=half, x3t=x3t: e.tensor_tensor(out=x3t[:, half * 512:(half + 1) * 512], in0=pb[:, :], in1=xres[:, t, half * 512:(half + 1) * 512], op=ALU.add),
                         reads=[pb, xres_b[t]], writes=[x3t])
                T.op(pool, lambda e: e.memset(ss3[:], 0.0), writes=[ss3])
                T.op(act, lambda e, x3t=x3t: e.activation(out=junk2[:], in_=x3t[:], func=AF.Square, accum_out=ss3[:]), reads=[x3t], writes=[ss3])
                T.op(act, lambda e: e.activation(out=rs3[:], in_=ss3[:], func=AF.Ln, scale=1.0 / 1024, bias=EPS), reads=[ss3], writes=[rs3])
                T.op(act, lambda e: e.activation(out=rs3[:], in_=rs3[:], func=AF.Exp, scale=-0.5), reads=[rs3], writes=[rs3])
                T.op(dve, lambda e, x3t=x3t, ott=ott: e.scalar_tensor_tensor(out=ott[:], in0=x3t[:], scalar=rs3[:, 0:1], in1=nwC[:, 1, :], op0=ALU.mult, op1=ALU.mult),
                     reads=[x3t, rs3, nwC], writes=[ott])
                T.dma(sp, osem[ocnt % 2], out[(t - 1) * 128:t * 128, :], ott[:], reads=[ott], writes=[])
                ocnt += 1
        for e_ in T.engs:
            T.wait_all(e_, [nwC, wdn, hfT, hfH, actT, uh, ss3, rs3] + wub + ub + cg + cv + sg + x3)
        T._wait(sp, [(s_, s_.count) for s_ in osem])
    return osem


def _consts():
    k = np.arange(128)
    return {
        "c_ident": np.eye(128, dtype=np.float32),
        "c_tri_le": (k[:, None] <= k[None, :]).astype(np.float32),
        "c_tri_gt": (k[:, None] > k[None, :]).astype(np.float32),
    }


def _pk(v, n):
    return np.ascontiguousarray(np.asarray(v, np.float32).reshape(n, 128).T)


def _bcast(v):
    v = np.asarray(v, np.float32).reshape(1, -1)
    return np.ascontiguousarray(np.broadcast_to(v, (128, v.shape[1])))


def prep_phase1(inp, S):
    L = 0
    x = np.asarray(inp["x"], np.float32)
    pos = np.asarray(inp["positions"], np.int32)
    w_in = np.asarray(inp["w_in"], np.float32)[L]
    conv_w = np.asarray(inp["ssd_conv_w"], np.float32)[L]
    conv_b = np.asarray(inp["ssd_conv_b"], np.float32)[L]
    w_q_up = np.asarray(inp["mla_w_q_up"], np.float32)[L]
    w_kv_up = np.asarray(inp["mla_w_kv_up"], np.float32)[L]
    inv_freq = (1.0 / (10000.0 ** (np.arange(0, 32, 2, dtype=np.float32) / np.float32(32)))).astype(np.float32)
    cst = _consts()
    maps = []
    for c in range(8):
        b, g = divmod(c, 4)
        xs_c = 1024 + g * 256 + np.arange(256)
        B_c = 1024 + 1024 + g * 128 + np.arange(128)
        C_c = 1024 + 1024 + 512 + g * 128 + np.arange(128)
        q_c = 3088 + np.arange(384)
        kv_c = 3088 + 384 + np.arange(256)
        z_c = g * 256 + np.arange(256)
        dt_c = 3072 + g * 4 + np.arange(4)
        kr_c = 3088 + 384 + 256 + np.arange(32)
        cols = np.concatenate([xs_c, B_c, C_c, q_c, kv_c, z_c, dt_c, kr_c])
        cch = np.concatenate([xs_c, B_c, C_c]) - 1024
        heads = 4 * g + np.arange(4)
        qcols = (heads[:, None] * 96 + np.arange(96)[None, :]).reshape(-1)
        kcols = (heads[:, None] * 128 + np.arange(64)[None, :]).reshape(-1)
        vcols = (heads[:, None] * 128 + 64 + np.arange(64)[None, :]).reshape(-1)
        m = {
            "x": np.ascontiguousarray(x[b, :S]),
            "pos": np.ascontiguousarray(pos[b, :S].reshape(S // 128, 128).T),
            "w_in": np.ascontiguousarray(w_in[:, cols]),
            "nmw": _pk(inp["norm_mix_w"][L], 8),
            "qnw": _pk(inp["mla_q_norm_w"][L], 3),
            "kvnw": _pk(inp["mla_kv_norm_w"][L], 2),
            "cw": np.ascontiguousarray(conv_w[:, cch].reshape(4, 4, 128).transpose(2, 1, 0)),
            "cb": np.ascontiguousarray(conv_b[cch].reshape(4, 128).T),
            "dtb": _bcast(np.asarray(inp["ssd_dt_bias"], np.float32)[L][heads]),
            "alog": _bcast(np.asarray(inp["ssd_a_log"], np.float32)[L][heads]),
            "dsk": _bcast(np.asarray(inp["ssd_d"], np.float32)[L][heads]),
            "snw": _bcast(np.asarray(inp["ssd_norm_w"], np.float32)[L][g * 256:(g + 1) * 256]),
            "invf": _bcast(inv_freq),
            "w_q": np.ascontiguousarray(w_q_up[:, qcols]),
            "w_kv": np.ascontiguousarray(w_kv_up[:, np.concatenate([kcols, vcols])]),
        }
        m.update(cst)
        maps.append(m)
    return maps


def build_p1(S, stages="RWABCDEFGO"):
    nc = bass.Bass("TRN2", target_bir_lowering=False)
    NT = S // 128
    shapes = {
        "x": ([S, 1024], F32), "pos": ([128, NT], I32), "w_in": ([1024, 1444], F32),
        "nmw": ([128, 8], F32), "qnw": ([128, 3], F32), "kvnw": ([128, 2], F32),
        "cw": ([128, 4, 4], F32), "cb": ([128, 4], F32), "dtb": ([128, 4], F32), "alog": ([128, 4], F32),
        "dsk": ([128, 4], F32), "snw": ([128, 256], F32), "invf": ([128, 16], F32),
        "w_q": ([384, 384], F32), "w_kv": ([256, 512], F32),
        "c_ident": ([128, 128], F32), "c_tri_le": ([128, 128], F32), "c_tri_gt": ([128, 128], F32),
    }
    d = {k: nc.dram_tensor(k, sh, dt, kind="ExternalInput").ap() for k, (sh, dt) in shapes.items()}
    yx = nc.dram_tensor("yx", [4, 512, 128 + S // 4], BF16, kind="ExternalOutput").ap()
    with ExitStack() as es:
        K = KB(nc, es)
        osems = phase1(K, S, d, yx, stages=stages)
        K.T._wait(K.T.sp, [(s, s.count) for s in osems])
        K.T.barrier(K.dsems)
        nc.all_engine_barrier()
    return nc


P2_SHAPES = lambda S: {
    "xh": ([128 + S // 4, 1024], F32), "mem": ([256, 1024], F32), "nw": ([128, 4, 1024], F32), "hflag": ([128, 1], F32),
    "fcw": ([128, 44, 3], F32), "fcb": ([128, 44], F32), "w_out": ([2048, 1024], F32),
    "w_mq": ([1024, 1024], F32), "w_mk": ([1024, 1024], F32), "w_mv": ([1024, 1024], F32), "w_mo": ([1024, 1024], F32),
    "w_up": ([22, 128, 2, 8, 128], F32), "w_dn": ([2816, 1024], F32), "c_ident": ([128, 128], F32),
}


def prep_phase2(inp, S):
    L = 0
    TQ = S // 4
    x = np.asarray(inp["x"], np.float32)
    mem = np.asarray(inp["mem"], np.float32)
    w_out = np.asarray(inp["w_out"], np.float32)[L]
    rows = []
    for g in range(4):
        for kb in range(4):
            base = g * 256 + kb * 128 if kb < 2 else 1024 + g * 256 + (kb - 2) * 128
            rows.append(base + np.arange(128))
    w_out_p = np.ascontiguousarray(w_out[np.concatenate(rows)])
    w_up = np.asarray(inp["w_ffn_up"], np.float32)[L]
    w_up_p = np.ascontiguousarray(w_up.reshape(8, 128, 2, 22, 128).transpose(3, 1, 2, 0, 4))
    fcw = np.asarray(inp["ffn_conv_w"], np.float32)[L]
    fcb = np.asarray(inp["ffn_conv_b"], np.float32)[L]
    nw = np.stack([np.asarray(inp[k], np.float32).reshape(-1) for k in ("norm_memx_w", "norm_mem_w", "norm_ffn_w", "norm_final_w")])
    common = {
        "nw": np.ascontiguousarray(np.broadcast_to(nw[None], (128, 4, 1024))),
        "fcw": np.ascontiguousarray(fcw.reshape(3, 44, 128).transpose(2, 1, 0)),
        "fcb": np.ascontiguousarray(fcb.reshape(44, 128).T),
        "w_out": w_out_p,
        "w_mq": np.asarray(inp["w_mem_q"], np.float32)[L], "w_mk": np.asarray(inp["w_mem_k"], np.float32)[L],
        "w_mv": np.asarray(inp["w_mem_v"], np.float32)[L], "w_mo": np.asarray(inp["w_mem_o"], np.float32)[L],
        "w_up": w_up_p, "w_dn": np.asarray(inp["w_ffn_down"], np.float32)[L],
        "c_ident": np.eye(128, dtype=np.float32),
    }
    maps = []
    for c in range(8):
        b, r = divmod(c, 4)
        xh = np.zeros((128 + TQ, 1024), np.float32)
        lo = r * TQ - 128
        if r == 0:
            xh[128:] = x[b, 0:TQ]
        else:
            xh[:] = x[b, lo:lo + 128 + TQ]
        m = {"xh": xh, "mem": np.ascontiguousarray(mem[b]), "hflag": np.full((128, 1), 0.0 if r == 0 else 1.0, np.float32)}
        m.update(common)
        maps.append(m)
    return maps


def build_p2(S):
    nc = bass.Bass("TRN2", target_bir_lowering=False)
    d = {k: nc.dram_tensor(k, sh, dt, kind="ExternalInput").ap() for k, (sh, dt) in P2_SHAPES(S).items()}
    yr = nc.dram_tensor("yr", [4, 512, 128 + S // 4], BF16, kind="ExternalInput").ap()
    out = nc.dram_tensor("out", [S // 4, 1024], F32, kind="ExternalOutput").ap()
    with ExitStack() as es:
        K = KB(nc, es)
        phase2(K, S, d, yr, out)
        K.T.barrier(K.dsems)
        nc.all_engine_barrier()
    return nc


P1_SHAPES = lambda S: {
    "x": ([S, 1024], F32), "pos": ([128, S // 128], I32), "w_in": ([1024, 1444], F32),
    "nmw": ([128, 8], F32), "qnw": ([128, 3], F32), "kvnw": ([128, 2], F32),
    "cw": ([128, 4, 4], F32), "cb": ([128, 4], F32), "dtb": ([128, 4], F32), "alog": ([128, 4], F32),
    "dsk": ([128, 4], F32), "snw": ([128, 256], F32), "invf": ([128, 16], F32),
    "w_q": ([384, 384], F32), "w_kv": ([256, 512], F32),
    "c_ident": ([128, 128], F32), "c_tri_le": ([128, 128], F32), "c_tri_gt": ([128, 128], F32),
}


def build_fused(S, coll="AllToAll"):
    nc = bass.Bass("TRN2", target_bir_lowering=False)
    shapes = dict(P1_SHAPES(S)); shapes.update(P2_SHAPES(S))
    d = {k: nc.dram_tensor(k, sh, dt, kind="ExternalInput").ap() for k, (sh, dt) in shapes.items()}
    out = nc.dram_tensor("out", [S // 4, 1024], F32, kind="ExternalOutput").ap()
    W = 128 + S // 4
    yx = nc.dram_tensor("yx_scr", [4, 512, W], BF16).ap()
    if coll == "AllToAll":
        yr = nc.dram_tensor("yr_scr", [4, 512, W], BF16).ap()
    else:
        yg = nc.dram_tensor("yg_scr", [4, 4, 512, W], BF16).ap()
    with ExitStack() as es:
        K = KB(nc, es)
        T = K.T
        with ExitStack() as es1:
            K.es = es1
            phase1(K, S, d, yx)
            T.barrier(K.dsems)
        K.es = es
        csem = K.dsem()
        groups = [[0, 1, 2, 3], [4, 5, 6, 7]]
        if coll == "AllToAll":
            inst = nc.gpsimd.collective_compute("AllToAll", op=ALU.bypass, replica_groups=groups,
                                                ins=[yx.rearrange("r f c -> (r f) c")], outs=[yr.rearrange("g f c -> (g f) c")])
        else:
            inst = nc.gpsimd.collective_compute("AllGather", op=ALU.bypass, replica_groups=groups,
                                                ins=[yx.rearrange("r f c -> (r f) c")], outs=[yg.rearrange("g r f c -> (g r f) c")])
        csem.count += 16
        inst.then_inc(csem.h, 16)
        for e_ in T.engs:
            T._wait(e_, [(csem, csem.count)])
        if coll == "AllToAll":
            phase2(K, S, d, yr, out)
        else:
            phase2(K, S, d, yg, out, gathered=True)
        T.barrier(K.dsems)
        nc.all_engine_barrier()
    return nc


def kernel_fused(**inputs):
    S = 8192
    TQ = S // 4
    inp = {k: np.asarray(v) for k, v in inputs.items()}
    m1 = prep_phase1(inp, S)
    m2 = prep_phase2(inp, S)
    maps = []
    for c in range(8):
        m = dict(m1[c]); m.update(m2[c]); maps.append(m)
    nc = build_fused(S)
    res = run_bass_kernel_spmd(nc, maps, core_ids=list(range(8)))
    out = np.zeros((2, S, 1024), np.float32)
    for c in range(8):
        b, r = divmod(c, 4)
        out[b, r * TQ:(r + 1) * TQ] = np.asarray(res.results[c]["out"])
    return out


def kernel(**inputs):
    S = 8192
    TQ = S // 4
    inp = {k: np.asarray(v) for k, v in inputs.items()}
    nc1 = build_p1(S)
    r1 = run_bass_kernel_spmd(nc1, prep_phase1(inp, S), core_ids=list(range(8)))
    yx = [np.asarray(r1.results[c]["yx"]) for c in range(8)]
    maps2 = prep_phase2(inp, S)
    for c in range(8):
        b, r = divmod(c, 4)
        maps2[c]["yr"] = np.ascontiguousarray(np.stack([yx[4 * b + g][r] for g in range(4)]))
    nc2 = build_p2(S)
    r2 = run_bass_kernel_spmd(nc2, maps2, core_ids=list(range(8)))
    out = np.zeros((2, S, 1024), np.float32)
    for c in range(8):
        b, r = divmod(c, 4)
        out[b, r * TQ:(r + 1) * TQ] = np.asarray(r2.results[c]["out"])
    return out
```
